# Optimizing a Trainium2 kernel written in Bass

```python
import jax, jax.numpy as jnp
from jax import lax
import numpy as np

D_MODEL = 1024
BATCH = 8
SEQ = 4096
DEPTH = 2

HEAD_DIM = 64
N_Q_HEADS = D_MODEL // HEAD_DIM
N_KV_HEADS = 4
GQA_GROUP = N_Q_HEADS // N_KV_HEADS
ATT_WIDTH = N_Q_HEADS * HEAD_DIM
KV_WIDTH = N_KV_HEADS * HEAD_DIM
ATT_IN = 2 * ATT_WIDTH + 2 * KV_WIDTH
WINDOW = 128
BLOCK = 128
ROPE_THETA = 10000.0
MASK_VALUE = -1e30

RWKV_HEAD = 64
RWKV_HEADS = D_MODEL // RWKV_HEAD
RWKV_WIDTH = RWKV_HEADS * RWKV_HEAD
DECAY_LORA = 64
AAA_LORA = 64
N_LERP = 6
GN_EPS = 64e-5

PLE_DIM = 256
NORM_EPS = 1e-6
N_ATTN_LAYERS = (DEPTH + 1) // 2
N_RWKV_LAYERS = DEPTH // 2

kernel_name = 'hybrid_swa_sink_rwkv7_ple'


def rms_norm(x, g):
    xf = x.astype(jnp.float32)
    y = xf * lax.rsqrt(jnp.mean(xf * xf, axis=-1, keepdims=True) + NORM_EPS)
    return (y * g.astype(jnp.float32)).astype(x.dtype)


def rope(x, pos):
    half = HEAD_DIM // 2
    inv = ROPE_THETA ** (-jnp.arange(half, dtype=jnp.float32) / half)
    ang = pos.astype(jnp.float32)[:, None] * inv[None, :]
    cos = jnp.cos(ang)[None, :, None, :]
    sin = jnp.sin(ang)[None, :, None, :]
    xf = x.astype(jnp.float32)
    x1, x2 = xf[..., :half], xf[..., half:]
    out = jnp.concatenate([x1 * cos - x2 * sin, x2 * cos + x1 * sin], axis=-1)
    return out.astype(x.dtype)


def swa_sink_mixer(h, w_in, b_in, sinks, w_out):
    B, T, _ = h.shape
    nb = T // BLOCK
    proj = h @ w_in + b_in
    q, k, v, z = jnp.split(proj, [ATT_WIDTH, ATT_WIDTH + KV_WIDTH, ATT_WIDTH + 2 * KV_WIDTH], axis=-1)
    pos = jnp.arange(T)
    q = rope(q.reshape(B, T, N_Q_HEADS, HEAD_DIM), pos)
    k = rope(k.reshape(B, T, N_KV_HEADS, HEAD_DIM), pos)
    v = v.reshape(B, T, N_KV_HEADS, HEAD_DIM)
    qb = q.reshape(B, nb, BLOCK, N_KV_HEADS, GQA_GROUP, HEAD_DIM)
    kb = k.reshape(B, nb, BLOCK, N_KV_HEADS, HEAD_DIM)
    vb = v.reshape(B, nb, BLOCK, N_KV_HEADS, HEAD_DIM)
    pad = ((0, 0), (1, 0), (0, 0), (0, 0), (0, 0))
    kw = jnp.concatenate([jnp.pad(kb, pad)[:, :-1], kb], axis=2)
    vw = jnp.concatenate([jnp.pad(vb, pad)[:, :-1], vb], axis=2)
    s = jnp.einsum('bnqkgd,bnskd->bnkgqs', qb, kw).astype(jnp.float32) * (HEAD_DIM ** -0.5)
    qi = jnp.arange(BLOCK)[:, None] + BLOCK
    si = jnp.arange(2 * BLOCK)[None, :]
    diff = qi - si
    band = (diff >= 0) & (diff < WINDOW)
    has_prev = (jnp.arange(nb)[:, None] > 0) | (jnp.arange(2 * BLOCK)[None, :] >= BLOCK)
    mask = band[None, :, :] & has_prev[:, None, :]
    s = jnp.where(mask[None, :, None, None, :, :], s, MASK_VALUE)
    sink = jnp.broadcast_to(sinks.astype(jnp.float32).reshape(N_KV_HEADS, GQA_GROUP)[None, None, :, :, None, None],
                            s.shape[:-1] + (1,))
    probs = jax.nn.softmax(jnp.concatenate([s, sink], axis=-1), axis=-1)[..., :-1]
    o = jnp.einsum('bnkgqs,bnskd->bnqkgd', probs.astype(vw.dtype), vw).reshape(B, T, ATT_WIDTH)
    return (o * jax.nn.silu(z)) @ w_out


def rwkv7_mixer(h, mu, w_in, w0, w1, w2, a0, a1, a2, k_k, k_a, r_k, gn_g, gn_b, w_out):
    B, T, C = h.shape
    H, N = RWKV_HEADS, RWKV_HEAD
    xx = jnp.pad(h, ((0, 0), (1, 0), (0, 0)))[:, :-1] - h
    xs = h[None] + xx[None] * mu[:, None, None, :]
    proj = jnp.einsum('cbtd,dce->cbte', xs[:4], w_in.reshape(C, 4, RWKV_WIDTH))
    r, k, v, z = proj[0], proj[1], proj[2], proj[3]
    xw, xa = xs[4], xs[5]
    w = -jax.nn.softplus(-(w0 + jnp.tanh(xw @ w1) @ w2)) - 0.5
    decay = jnp.exp(-jnp.exp(w.astype(jnp.float32)))
    a = jax.nn.sigmoid(a0 + (xa @ a1) @ a2)
    r4 = r.reshape(B, T, H, N).astype(jnp.float32)
    k4 = k.reshape(B, T, H, N).astype(jnp.float32)
    v4 = v.reshape(B, T, H, N).astype(jnp.float32)
    a4 = a.reshape(B, T, H, N).astype(jnp.float32)
    d4 = decay.reshape(B, T, H, N)
    kk = k4 * k_k.reshape(H, N).astype(jnp.float32)
    kk = kk / jnp.maximum(jnp.sqrt(jnp.sum(kk * kk, axis=-1, keepdims=True)), 1e-12)
    k4 = k4 * (1.0 + (a4 - 1.0) * k_a.reshape(H, N).astype(jnp.float32))

    def step(S, inp):
        r_t, d_t, k_t, v_t, kk_t, a_t = inp
        sa = jnp.einsum('bhvk,bhk->bhv', S, -kk_t)
        S = S * d_t[:, :, None, :] + sa[..., None] * (kk_t * a_t)[:, :, None, :] + v_t[..., None] * k_t[:, :, None, :]
        y = jnp.einsum('bhvk,bhk->bhv', S, r_t)
        return S, y

    seq_in = tuple(jnp.swapaxes(t, 0, 1) for t in (r4, d4, k4, v4, kk, a4))
    S0 = jnp.zeros((B, H, N, N), jnp.float32)
    _, ys = lax.scan(step, S0, seq_in)
    y = jnp.swapaxes(ys, 0, 1)
    mean = jnp.mean(y, axis=-1, keepdims=True)
    var = jnp.mean(jnp.square(y - mean), axis=-1, keepdims=True)
    y = ((y - mean) * lax.rsqrt(var + GN_EPS)).reshape(B, T, C) * gn_g.astype(jnp.float32) + gn_b.astype(jnp.float32)
    bonus = jnp.sum(r4 * k4 * r_k.astype(jnp.float32), axis=-1, keepdims=True) * v4
    y = (y + bonus.reshape(B, T, C)).astype(h.dtype)
    return (y * jax.nn.silu(z)) @ w_out


def setup_inputs(seed: int = 0) -> dict:
    key = jax.random.key(seed)
    ks = jax.random.split(key, 32)
    f = jnp.float32
    nA, nR = N_ATTN_LAYERS, N_RWKV_LAYERS
    C = D_MODEL

    def nrm(k, shape, scale):
        return jax.random.normal(k, shape, f) * scale

    return {
        'x': jax.random.normal(ks[0], (BATCH, SEQ, C), f),
        'p': jax.random.normal(ks[1], (DEPTH, BATCH, SEQ, PLE_DIM), f),
        'norm_g': 1.0 + nrm(ks[2], (DEPTH, C), 0.02),
        'attn_w_in': nrm(ks[3], (nA, C, ATT_IN), C ** -0.5),
        'attn_b_in': nrm(ks[4], (nA, ATT_IN), 0.02),
        'attn_sinks': nrm(ks[5], (nA, N_Q_HEADS), 0.5),
        'attn_w_out': nrm(ks[6], (nA, ATT_WIDTH, C), ATT_WIDTH ** -0.5),
        'rwkv_mu': jax.random.uniform(ks[7], (nR, N_LERP, C), f),
        'rwkv_w_in': nrm(ks[8], (nR, C, 4 * RWKV_WIDTH), C ** -0.5),
        'rwkv_w0': -2.0 + nrm(ks[9], (nR, RWKV_WIDTH), 1.0),
        'rwkv_w1': nrm(ks[10], (nR, C, DECAY_LORA), C ** -0.5),
        'rwkv_w2': nrm(ks[11], (nR, DECAY_LORA, RWKV_WIDTH), 0.1 * DECAY_LORA ** -0.5),
        'rwkv_a0': nrm(ks[12], (nR, RWKV_WIDTH), 0.1),
        'rwkv_a1': nrm(ks[13], (nR, C, AAA_LORA), C ** -0.5),
        'rwkv_a2': nrm(ks[14], (nR, AAA_LORA, RWKV_WIDTH), AAA_LORA ** -0.5),
        'rwkv_k_k': 0.85 + nrm(ks[15], (nR, RWKV_WIDTH), 0.05),
        'rwkv_k_a': 1.0 + nrm(ks[16], (nR, RWKV_WIDTH), 0.05),
        'rwkv_r_k': nrm(ks[17], (nR, RWKV_HEADS, RWKV_HEAD), 0.1),
        'rwkv_gn_g': 1.0 + nrm(ks[18], (nR, RWKV_WIDTH), 0.02),
        'rwkv_gn_b': nrm(ks[19], (nR, RWKV_WIDTH), 0.02),
        'rwkv_w_out': nrm(ks[20], (nR, RWKV_WIDTH, C), RWKV_WIDTH ** -0.5),
        'ple_w_proj': nrm(ks[21], (DEPTH, PLE_DIM, C), PLE_DIM ** -0.5),
        'ple_w_gate': nrm(ks[22], (DEPTH, C, C), C ** -0.5),
        'final_norm_g': 1.0 + nrm(ks[23], (C,), 0.02),
    }


def reference(x, p, norm_g, attn_w_in, attn_b_in, attn_sinks, attn_w_out,
              rwkv_mu, rwkv_w_in, rwkv_w0, rwkv_w1, rwkv_w2, rwkv_a0, rwkv_a1, rwkv_a2,
              rwkv_k_k, rwkv_k_a, rwkv_r_k, rwkv_gn_g, rwkv_gn_b, rwkv_w_out,
              ple_w_proj, ple_w_gate, final_norm_g):
    h = x
    for i in range(DEPTH):
        hn = rms_norm(h, norm_g[i])
        j = i // 2
        if i % 2 == 0:
            m = swa_sink_mixer(hn, attn_w_in[j], attn_b_in[j], attn_sinks[j], attn_w_out[j])
        else:
            m = rwkv7_mixer(hn, rwkv_mu[j], rwkv_w_in[j], rwkv_w0[j], rwkv_w1[j], rwkv_w2[j],
                            rwkv_a0[j], rwkv_a1[j], rwkv_a2[j], rwkv_k_k[j], rwkv_k_a[j], rwkv_r_k[j],
                            rwkv_gn_g[j], rwkv_gn_b[j], rwkv_w_out[j])
        h = h + m
        h = h + jax.nn.sigmoid(h @ ple_w_gate[i]) * (p[i] @ ple_w_proj[i])
    return rms_norm(h, final_norm_g)
```

```python
import contextlib
import numpy as np
import concourse.bass as bass
import concourse.mybir as mybir
from concourse.bass_utils import run_bass_kernel_spmd

F32 = mybir.dt.float32
BF16 = mybir.dt.bfloat16
AF = mybir.ActivationFunctionType
ALU = mybir.AluOpType
AX = mybir.AxisListType

D = 1024
NH = 16
HD = 64
NKV = 4
PLE = 256
EPOCH = 12000
DECAY_K = -float(np.exp(-0.5))


class Prog:
    ENG = ("pe", "act", "dve", "pool", "sp")

    def __init__(self, nc, ndma=8):
        self.nc = nc
        self.q = {e: [] for e in self.ENG}
        self.cnt = {e: 0 for e in self.ENG}
        self.sems = {}
        self.seen = {e: {} for e in self.ENG}
        self.lastw = {}
        self.readers = {}
        self.ndma = ndma
        self.dcount = {}
        self.alias = {}

    def sem(self, key):
        if key not in self.sems:
            self.sems[key] = self.nc.alloc_semaphore(name="s%d" % len(self.sems))
        return self.sems[key]

    def _canon(self, names):
        return list(dict.fromkeys(self.alias.get(n, n) for n in names))

    def _deps(self, reads, writes):
        deps = []
        for r in reads:
            if r in self.lastw:
                deps.append((self.lastw[r], "raw"))
        for w in writes:
            if w in self.lastw:
                deps.append((self.lastw[w], "waw"))
            for key, (val, teng) in self.readers.get(w, {}).items():
                deps.append(((key, val, teng), "war"))
        return deps

    def _waits(self, eng, deps):
        need = {}
        for (tok, kind) in deps:
            key, val, teng = tok
            if teng == eng and eng == "pe":
                continue
            if need.get(key, 0) < val:
                need[key] = val
        waits = []
        for key, val in need.items():
            if self.seen[eng].get(key, 0) < val:
                self.seen[eng][key] = val
                waits.append((key, val))
        return waits

    def _commit(self, tok, reads, writes):
        key, val, teng = tok
        for w in writes:
            self.lastw[w] = tok
            self.readers[w] = {}
        for r in reads:
            if r in writes:
                continue
            d = self.readers.setdefault(r, {})
            if d.get(key, (0, None))[0] < val:
                d[key] = (val, teng)

    def op(self, eng, fn, reads=(), writes=(), serial=False):
        reads, writes = self._canon(reads), self._canon(writes)
        writes = writes + [r for r in reads if r.startswith("ps") and r not in writes]
        deps = self._deps(reads, writes)
        waits = self._waits(eng, deps)
        n = self.cnt[eng]
        if serial and n > 0 and self.seen[eng].get(("c", eng), 0) < n:
            self.seen[eng][("c", eng)] = n
            waits.append((("c", eng), n))
        self.cnt[eng] = n + 1
        key = ("c", eng)
        self.q[eng].append((fn, waits, key, n + 1))
        self._commit((key, n + 1, eng), reads, writes)

    def dma(self, out, in_, reads=(), writes=(), eng="sp", slow=False):
        j = self.dcount.get(eng, 0)
        self.dcount[eng] = j + 1
        slot = j % self.ndma
        key = ("d", eng, slot)
        prev = (j // self.ndma) * 16
        reads, writes = self._canon(reads), self._canon(writes)
        deps = self._deps(reads, writes)
        if prev > 0:
            deps.append(((key, prev, "dma"), "raw"))
        waits = self._waits(eng, deps)
        if slow:
            self.q[eng].append((lambda e: e.dma_start(out=out, in_=in_, allow_slow_non_contiguous=True), waits, key, 16))
        else:
            self.q[eng].append((lambda e: e.dma_start(out=out, in_=in_), waits, key, 16))
        self._commit((key, prev + 16, "dma"), reads, writes)

    def wait_all(self, eng, resources):
        deps = [(self.lastw[r], "raw") for r in self._canon(resources) if r in self.lastw]
        waits = self._waits(eng, deps)
        self.q[eng].append((None, waits, None, 0))

    def barrier(self):
        latest = {}
        for e in self.ENG:
            if self.cnt[e] > 0:
                latest[("c", e)] = self.cnt[e]
        for ring, j in self.dcount.items():
            for slot in range(min(j, self.ndma)):
                cntslot = (j - 1 - slot) // self.ndma + 1
                latest[("d", ring, slot)] = cntslot * 16
        for e in self.ENG:
            waits = []
            for key, val in latest.items():
                if self.seen[e].get(key, 0) < val:
                    self.seen[e][key] = val
                    waits.append((key, val))
            self.q[e].append((None, waits, None, 0))

    def emit(self):
        nc = self.nc
        self.barrier()
        targets = {e: set() for e in self.ENG}
        for e in self.ENG:
            for (fn, waits, key, inc) in self.q[e]:
                for (k, v) in waits:
                    if k[0] == "c":
                        targets[k[1]].add(v)
        rank = {e: {v: i + 1 for i, v in enumerate(sorted(targets[e]))} for e in self.ENG}

        def csem(e, v):
            m = rank[e][v]
            return self.sem(("c", e, (m - 1) // EPOCH)), (m - 1) % EPOCH + 1

        plan = {e: [] for e in self.ENG}
        for e in self.ENG:
            for (fn, waits, key, inc) in self.q[e]:
                w2 = []
                for (k, v) in waits:
                    if k[0] == "c":
                        w2.append(csem(k[1], v))
                    else:
                        w2.append((self.sem(k), v))
                if fn is None:
                    plan[e].append((None, w2, None, 0))
                elif key[0] == "c":
                    if inc in rank[e]:
                        plan[e].append((fn, w2, csem(e, inc)[0], 1))
                    else:
                        plan[e].append((fn, w2, None, 0))
                else:
                    plan[e].append((fn, w2, self.sem(key), 16))

        def run(eng, name):
            for (fn, waits, sem, inc) in plan[name]:
                for (sm, v) in waits:
                    eng.wait_ge(sm, v)
                if fn is not None:
                    ins = fn(eng)
                    if sem is not None:
                        ins.then_inc(sem, inc)

        with nc.Block() as block:
            @block.tensor
            def _(eng):
                run(eng, "pe")

            @block.scalar
            def _(eng):
                run(eng, "act")

            @block.vector
            def _(eng):
                run(eng, "dve")

            @block.gpsimd
            def _(eng):
                run(eng, "pool")

            @block.sync
            def _(eng):
                run(eng, "sp")
        self.q = {e: [] for e in self.ENG}
        self.stats = {e: (self.cnt[e], len(rank[e])) for e in self.ENG}

    def mm(self, out, lhsT, rhs, start, stop, R, W, serial=False):
        self.op("pe", lambda e: e.matmul(out, lhsT=lhsT, rhs=rhs, start=start, stop=stop), R, W, serial=serial)

    def tr(self, out, in_, ident, R, W):
        self.op("pe", lambda e: e.transpose(out=out, in_=in_, identity=ident), R, W)

    def tt(self, eng, out, a, b, op, R, W):
        self.op(eng, lambda e: e.tensor_tensor(out=out, in0=a, in1=b, op=op), R, W)

    def ts(self, eng, out, a, s1, op0, R, W, s2=None, op1=None):
        if op1 is None:
            self.op(eng, lambda e: e.tensor_scalar(out=out, in0=a, scalar1=s1, scalar2=None, op0=op0), R, W)
        else:
            self.op(eng, lambda e: e.tensor_scalar(out=out, in0=a, scalar1=s1, scalar2=s2, op0=op0, op1=op1), R, W)

    def stt(self, eng, out, a, s, b, op0, op1, R, W):
        self.op(eng, lambda e: e.scalar_tensor_tensor(out=out, in0=a, scalar=s, in1=b, op0=op0, op1=op1), R, W)

    def act(self, out, in_, func, R, W, scale=1.0, bias=0.0, accum=None):
        if accum is None:
            self.op("act", lambda e: e.activation(out=out, in_=in_, func=func, bias=bias, scale=scale), R, W)
        else:
            self.op("act", lambda e: e.activation(out=out, in_=in_, func=func, bias=bias, scale=scale,
                                                  accum_out=accum), R, W)

    def cp(self, eng, out, in_, R, W):
        if eng == "act":
            self.op("act", lambda e: e.copy(out=out, in_=in_), R, W)
        else:
            self.op(eng, lambda e: e.tensor_copy(out=out, in_=in_), R, W)

    def memset(self, eng, ap, val, W):
        self.op(eng, lambda e: e.memset(ap, val), (), W)


def build_program(T, phases=("l0", "l1a", "l1b"), debug=False, stop=0):
    NT = T // 128
    nc = bass.Bass("TRN2", target_bir_lowering=False)

    def din(name, shape):
        return nc.dram_tensor(name, list(shape), F32, kind="ExternalInput").ap()

    x = din("x", [T, D])
    p0 = din("p0", [T, PLE])
    p1 = din("p1", [T, PLE])
    norm_g = din("norm_g", [2, D])
    attn_w_in = din("attn_w_in", [D, 2560])
    attn_b_in = din("attn_b_in", [2560])
    attn_sinks = din("attn_sinks", [NH])
    attn_w_out = din("attn_w_out", [D, D])
    rwkv_mu = din("rwkv_mu", [6, D])
    rwkv_w_in = din("rwkv_w_in", [D, 4 * D])
    rwkv_w0 = din("rwkv_w0", [D])
    rwkv_w1 = din("rwkv_w1", [D, 64])
    rwkv_w2 = din("rwkv_w2", [64, D])
    rwkv_a0 = din("rwkv_a0", [D])
    rwkv_a1 = din("rwkv_a1", [D, 64])
    rwkv_a2 = din("rwkv_a2", [64, D])
    rwkv_k_k = din("rwkv_k_k", [D])
    rwkv_k_a = din("rwkv_k_a", [D])
    rwkv_r_k = din("rwkv_r_k", [D])
    rwkv_gn_g = din("rwkv_gn_g", [D])
    rwkv_gn_b = din("rwkv_gn_b", [D])
    rwkv_w_out = din("rwkv_w_out", [D, D])
    ple_w_proj = din("ple_w_proj", [2, PLE, D])
    ple_w_gate = din("ple_w_gate", [2, D, D])
    final_norm_g = din("final_norm_g", [D])
    c_ident = din("c_ident", [128, 128])
    c_masks = din("c_masks", [128, 4, 128])
    c_rope = din("c_rope", [T, 2, 32])
    out = nc.dram_tensor("out", [T, D], F32, kind="ExternalOutput").ap()
    sk = "ExternalOutput" if debug else "Internal"
    H1 = nc.dram_tensor("h1s", [T, D], F32, kind=sk).ap()
    H2 = nc.dram_tensor("h2s", [T, D], F32, kind=sk).ap()

    P = Prog(nc)
    glob = contextlib.ExitStack()

    def galloc(name, shape, dt):
        return glob.enter_context(nc.sbuf_tensor(name, list(shape), dt))

    psum = glob.enter_context(nc.psum_tensor("psum", [128, 8, 512], F32))

    def bank(b, n=1):
        return psum[:, b:b + n, :]

    def bres(b, n=1):
        return ["ps%d" % i for i in range(b, b + n)]

    wbig = galloc("wbig", [128, 44032], BF16)
    stg_tiles = []
    gcol = galloc("gcol", [128, 2, 8], F32)
    identf = galloc("identf", [128, 128], F32)
    identb = galloc("identb", [128, 128], BF16)
    maskf = galloc("maskf", [128, 4, 128], F32)
    maskb = galloc("maskb", [128, 4, 128], BF16)
    arena_cols = (nc.sbuf_bytes_remaining - 1024) // 4
    arena = galloc("arena", [128, arena_cols], F32)
    aoff = [0]

    def al(name, shape, dt):
        shape = list(shape)
        esz = 4 if dt == F32 else 2
        n = 1
        for d in shape[1:]:
            n *= d
        nbytes = (n * esz + 31) // 32 * 32
        c0 = aoff[0] // 4
        aoff[0] += nbytes
        assert aoff[0] <= arena_cols * 4, ("arena overflow", name, aoff[0], arena_cols * 4)
        ap = arena[0:shape[0], c0:c0 + nbytes // 4]
        if dt != F32:
            ap = ap.bitcast(dt)
        ap = ap[:, 0:n]
        if len(shape) == 3:
            ap = ap.rearrange("p (a b) -> p a b", a=shape[1])
        elif len(shape) == 4:
            ap = ap.rearrange("p (a b c) -> p a b c", a=shape[1], b=shape[2])
        return ap
    P.dma(identf[:], c_ident, (), ["identf"])
    P.dma(maskf[:], c_masks, (), ["maskf"])
    P.cp("dve", identb[:], identf[:], ["identf"], ["identb"])
    P.cp("dve", maskb[:], maskf[:], ["maskf"], ["maskb"])
    for l in range(2):
        P.dma(gcol[:, l, :], norm_g[l].rearrange("(c p) -> p c", p=128), (), ["gcol"], slow=True)

    MUs, MUi, MLs, BDm = (maskb[:, i, :] for i in range(4))
    stg_i = [0]

    def load_w(dst, src, ncols, rows=128, gscale=None, R=("gcol",)):
        for c0 in range(0, ncols, 1024):
            cw = min(1024, ncols - c0)
            i = stg_i[0] % 2
            stg_i[0] += 1
            s, sr = stg_tiles[i]
            P.dma(s[0:rows, 0:cw], src[:, c0:c0 + cw], (), [sr])
            if gscale is None:
                if i == 0:
                    P.cp("act", dst[:, c0:c0 + cw], s[0:rows, 0:cw], [sr], ["W"])
                else:
                    P.cp("dve", dst[:, c0:c0 + cw], s[0:rows, 0:cw], [sr], ["W"])
            else:
                if i == 0:
                    P.act(dst[:, c0:c0 + cw], s[0:rows, 0:cw], AF.Copy, [sr] + list(R), ["W"], scale=gscale)
                else:
                    P.ts("dve", dst[:, c0:c0 + cw], s[0:rows, 0:cw], gscale, ALU.mult, [sr] + list(R), ["W"])

    def load_wk(dst3, src, ncols, nk, gl=None):
        for kc in range(nk):
            load_w(dst3[:, kc, :], src[kc * 128:(kc + 1) * 128, :], ncols,
                   gscale=(gcol[:, gl, kc:kc + 1] if gl is not None else None))

    def rms_stats(eng_h, h_ap, Rh, ssq, scratch, scr_res):
        P.act(scratch, h_ap, AF.Square, Rh, [scr_res, "ssq"], accum=ssq[:, 0:1])
        P.act(ssq[:, 1:2], ssq[:, 0:1], AF.Sqrt, ["ssq"], ["ssq"], scale=1.0 / D, bias=1e-6)
        P.op("dve", lambda e: e.reciprocal(out=ssq[:, 2:3], in_=ssq[:, 1:2]), ["ssq"], ["ssq"])

    def transpose8(src_bf, Rsrc, dstT, Wdst, b, evac_eng="act"):
        pb = bank(b)[:, 0, :].bitcast(BF16).rearrange("p (c t) -> p c t", c=8)
        for c in range(8):
            P.tr(pb[:, c, :], src_bf[:, c * 128:(c + 1) * 128], identb[:], Rsrc + ["identb"], bres(b))
        P.cp(evac_eng, dstT, pb, bres(b), Wdst)

    def ple_part(hf, Rh, pt, Rp, wg, wp, wk, b_gate, b_p, b_tr):
        hb, hT, pb_, pT, sg = wk["hb"], wk["hT"], wk["pb"], wk["pT"], wk["sg"]
        P.cp("pool", hb[:], hf, Rh, ["hb"])
        transpose8(hb[:], ["hb"], hT[:], ["hT"], b_tr)
        g2 = bank(b_gate, 2)
        for half in range(2):
            for kc in range(8):
                P.mm(g2[:, half, :], hT[:, kc, :], wg[:, kc, half * 512:(half + 1) * 512], kc == 0, kc == 7,
                     ["hT", "W"], bres(b_gate + half))
        P.act(sg[:].rearrange("p (a n) -> p a n", a=2), g2, AF.Sigmoid, bres(b_gate, 2), ["sg"])
        P.cp("pool", pb_[:], pt, Rp, ["pb"])
        pbk = bank(b_tr)[:, 0, :].bitcast(BF16).rearrange("p (c t) -> p c t", c=8)
        for c in range(2):
            P.tr(pbk[:, c, :], pb_[:, c * 128:(c + 1) * 128], identb[:], ["pb", "identb"], bres(b_tr))
        P.cp("act", pT[:], pbk[:, 0:2, :], bres(b_tr), ["pT"])
        q2 = bank(b_p, 2)
        for half in range(2):
            for kc in range(2):
                P.mm(q2[:, half, :], pT[:, kc, :], wp[:, kc, half * 512:(half + 1) * 512], kc == 0, kc == 1,
                     ["pT", "W"], bres(b_p + half))
        P.tt("dve", sg[:].rearrange("p (a n) -> p a n", a=2), sg[:].rearrange("p (a n) -> p a n", a=2), q2,
             ALU.mult, ["sg"] + bres(b_p, 2), ["sg"])
        P.tt("pool", hf, hf, sg[:], ALU.add, Rh + ["sg"], Rh)

    if "l0" in phases:
        with contextlib.ExitStack() as st:
            aoff[0] = 0
            w_in = wbig[:, 0:20480].rearrange("p (c n) -> p c n", c=8)
            w_out = wbig[:, 20480:28672].rearrange("p (c n) -> p c n", c=8)
            w_g = wbig[:, 28672:36864].rearrange("p (c n) -> p c n", c=8)
            w_p = wbig[:, 36864:38912].rearrange("p (c n) -> p c n", c=2)
            hbuf = [al("hbuf%d" % i, [128, D], F32) for i in range(2)]
            stg_tiles[:] = [(hbuf[i], "hbuf%d" % i) for i in range(2)]
            load_wk(w_in, attn_w_in, 2560, 8, gl=0)
            load_wk(w_out, attn_w_out, 1024, 8)
            load_wk(w_g, ple_w_gate[0], 1024, 8)
            load_wk(w_p, ple_w_proj[0], 1024, 2)

            bias_bc = al("bias_bc", [128, 2560], F32)
            esink = al("esink", [128, NH], F32)
            P.dma(bias_bc[:], attn_b_in.partition_broadcast(128), (), ["bias_bc"])
            P.dma(esink[:], attn_sinks.partition_broadcast(128), (), ["esink"])
            P.act(esink[:], esink[:], AF.Exp, ["esink"], ["esink"])

            ptl = [al("ptl%d" % i, [128, PLE], F32) for i in range(2)]
            cs = [al("cs%d" % i, [128, 2, 32], F32) for i in range(2)]
            ssq = al("ssq", [128, 4], F32)
            junk = al("junk", [128, D], F32)
            hn = al("hn", [128, D], BF16)
            hnT = al("hnT", [128, 8, 128], BF16)
            qkf = al("qkf", [128, 1280], F32)
            rtmp = al("rtmp", [128, 20, 32], F32)
            rtmp2 = al("rtmp2", [128, 20, 32], F32)
            qkr = al("qkr", [128, 1280], BF16)
            qT = al("qT", [64, 16, 128], BF16)
            kT = [al("kT%d" % i, [64, 4, 128], BF16) for i in range(2)]
            vaug = [al("vaug%d" % i, [128, 4, 65], BF16) for i in range(2)]
            sz = al("sz", [128, D], F32)
            pT_ = [al("pTb%d" % i, [128, 4, 512], BF16) for i in range(2)]
            den = al("den", [128, NH], F32)
            of = al("of", [128, NH, 64], F32)
            gt = al("gt", [128, D], BF16)
            gT = al("gT", [128, 8, 128], BF16)
            wk = dict(hb=al("hb", [128, D], BF16), hT=al("hT", [128, 8, 128], BF16), pb=al("pb", [128, PLE], BF16),
                      pT=al("pT", [128, 2, 128], BF16), sg=al("sg", [128, D], F32))
            for i in range(2):
                P.memset("pool", vaug[i][:], 1.0, ["vaug%d" % i])

            def loads(t):
                par = t % 2
                P.dma(hbuf[par][:], x[t * 128:(t + 1) * 128, :], (), ["hbuf%d" % par])
                P.dma(ptl[par][:], p0[t * 128:(t + 1) * 128, :], (), ["ptl%d" % par])
                P.dma(cs[par][:], c_rope[t * 128:(t + 1) * 128], (), ["cs%d" % par])

            loads(0)
            for t in range(NT):
                par = t % 2
                if t + 1 < NT:
                    loads(t + 1)
                h = hbuf[par]
                Rh = ["hbuf%d" % par]
                rms_stats("act", h[:], Rh, ssq, junk[:], "junk")
                P.ts("dve", hn[:], h[:], ssq[:, 2:3], ALU.mult, Rh + ["ssq"], ["hn"])
                transpose8(hn[:], ["hn"], hnT[:], ["hnT"], 5)
                for blk in range(5):
                    for kc in range(8):
                        P.mm(bank(blk)[:, 0, :], hnT[:, kc, :], w_in[:, kc, blk * 512:(blk + 1) * 512], kc == 0, kc == 7,
                             ["hnT", "W"], bres(blk))
                P.tt("dve", qkf[:, 0:1024].rearrange("p (a n) -> p a n", a=2), bank(0, 2),
                     bias_bc[:, 0:1024].rearrange("p (a n) -> p a n", a=2), ALU.add, bres(0, 2) + ["bias_bc"], ["qkf"])
                P.tt("dve", qkf[:, 1024:1280], bank(2)[:, 0, 0:256], bias_bc[:, 1024:1280], ALU.add,
                     bres(2) + ["bias_bc"], ["qkf"])
                P.tt("dve", vaug[par][:, :, 0:64], bank(2)[:, 0, 256:512].rearrange("p (g d) -> p g d", g=4),
                     bias_bc[:, 1280:1536].rearrange("p (g d) -> p g d", g=4), ALU.add,
                     bres(2) + ["bias_bc"], ["vaug%d" % par])
                P.tt("dve", sz[:].rearrange("p (a n) -> p a n", a=2), bank(3, 2),
                     bias_bc[:, 1536:2560].rearrange("p (a n) -> p a n", a=2), ALU.add, bres(3, 2) + ["bias_bc"], ["sz"])
                P.act(sz[:], sz[:], AF.Silu, ["sz"], ["sz"])
                qv = qkf[:].rearrange("p (h t d) -> p h t d", h=20, t=2)
                ov = qkr[:].rearrange("p (h t d) -> p h t d", h=20, t=2)
                cosb = cs[par][:, 0, :].unsqueeze(1).broadcast_to([128, 20, 32])
                sinb = cs[par][:, 1, :].unsqueeze(1).broadcast_to([128, 20, 32])
                Rc = ["qkf", "cs%d" % par]
                P.tt("pool", rtmp[:], qv[:, :, 1, :], sinb, ALU.mult, Rc, ["rtmp"])
                P.tt("dve", rtmp2[:], qv[:, :, 0, :], cosb, ALU.mult, Rc, ["rtmp2"])
                P.tt("dve", ov[:, :, 0, :], rtmp2[:], rtmp[:], ALU.subtract, ["rtmp", "rtmp2"], ["qkr"])
                P.tt("pool", rtmp[:], qv[:, :, 0, :], sinb, ALU.mult, Rc, ["rtmp"])
                P.tt("dve", rtmp2[:], qv[:, :, 1, :], cosb, ALU.mult, Rc, ["rtmp2"])
                P.tt("dve", ov[:, :, 1, :], rtmp2[:], rtmp[:], ALU.add, ["rtmp", "rtmp2"], ["qkr"])
                for grp in range(3):
                    b = 5 + grp
                    pbk = bank(b)[0:64, 0, :].bitcast(BF16).rearrange("p (c t) -> p c t", c=8)
                    h0 = grp * 8
                    nh = min(8, 20 - h0)
                    for i in range(nh):
                        hh = h0 + i
                        P.tr(pbk[:, i, :], qkr[:, hh * 64:(hh + 1) * 64], identb[:], ["qkr", "identb"], bres(b))
                    if grp < 2:
                        P.cp("act", qT[:, h0:h0 + 8, :], pbk, bres(b), ["qT"])
                    else:
                        P.cp("act", kT[par][:, :, :], pbk[:, 0:4, :], bres(b), ["kT%d" % par])
                blocks = [(1, par, MUi)] if t == 0 else [(0, 1 - par, MLs), (1, par, MUi)]
                si = 0
                for g in range(4):
                    for (bi, kp, msk) in blocks:
                        b = 6 + (si % 2)
                        si += 1
                        P.mm(bank(b)[:, 0, :], kT[kp][:, g, :], qT[:, 4 * g:4 * g + 4, :], True, True,
                             ["kT%d" % kp, "qT"], bres(b))
                        pr = "pTb%d_%d" % (bi, g)
                        P.act(pT_[bi][:, g, :], bank(b)[:, 0, :], AF.Exp, bres(b), [pr], scale=HD ** -0.5)
                        pv = pT_[bi][:, g, :].rearrange("p (h q) -> p h q", h=4)
                        P.tt("pool" if bi == 0 else "dve", pv, pv, msk.unsqueeze(1).broadcast_to([128, 4, 128]),
                             ALU.mult, [pr, "maskb"], [pr])
                for hh in range(NH):
                    g = hh // 4
                    b = hh // 7
                    o_ps = bank(b)[:, 0, (hh % 7) * 65:(hh % 7) * 65 + 65]
                    for j, (bi, kp, msk) in enumerate(blocks):
                        P.mm(o_ps, pT_[bi][:, g, (hh % 4) * 128:(hh % 4) * 128 + 128], vaug[kp][:, g, :],
                             j == 0, j == len(blocks) - 1, ["pTb%d_%d" % (bi, g), "vaug%d" % kp], bres(b))
                for b in range(3):
                    n = 7 if b < 2 else 2
                    ob = bank(b)[:, 0, 0:n * 65].rearrange("p (h d) -> p h d", h=n)
                    P.tt("dve", den[:, b * 7:b * 7 + n].unsqueeze(2), ob[:, :, 64:65], esink[:, b * 7:b * 7 + n].unsqueeze(2),
                         ALU.add, bres(b) + ["esink"], ["den"])
                P.op("dve", lambda e: e.reciprocal(out=den[:], in_=den[:]), ["den"], ["den"])
                for b in range(3):
                    n = 7 if b < 2 else 2
                    ob = bank(b)[:, 0, 0:n * 65].rearrange("p (h d) -> p h d", h=n)
                    P.tt("dve", of[:, b * 7:b * 7 + n, :], ob[:, :, 0:64],
                         den[:, b * 7:b * 7 + n].unsqueeze(2).broadcast_to([128, n, 64]), ALU.mult,
                         bres(b) + ["den"], ["of"])
                P.tt("pool", gt[:], of[:].rearrange("p h d -> p (h d)"), sz[:], ALU.mult, ["of", "sz"], ["gt"])
                transpose8(gt[:], ["gt"], gT[:], ["gT"], 5)
                m2 = bank(3, 2)
                for half in range(2):
                    for kc in range(8):
                        P.mm(m2[:, half, :], gT[:, kc, :], w_out[:, kc, half * 512:(half + 1) * 512], kc == 0, kc == 7,
                             ["gT", "W"], bres(3 + half))
                P.tt("dve", h[:].rearrange("p (a n) -> p a n", a=2), h[:].rearrange("p (a n) -> p a n", a=2), m2, ALU.add,
                     Rh + bres(3, 2), Rh)
                ple_part(h[:], Rh, ptl[par][:], ["ptl%d" % par], w_g, w_p, wk, 0, 6, 5)
                P.dma(H1[t * 128:(t + 1) * 128, :], h[:], Rh, ["H1_%d" % t])
            P.barrier()

    if "l1a" in phases:
        with contextlib.ExitStack() as st:
            aoff[0] = 0
            rw_in = wbig[:, 0:32768].rearrange("p (c n) -> p c n", c=8)
            rw_out = wbig[:, 32768:40960].rearrange("p (c n) -> p c n", c=8)
            w1b = wbig[:, 40960:41472].rearrange("p (c n) -> p c n", c=8)
            a1b = wbig[:, 41472:41984].rearrange("p (c n) -> p c n", c=8)
            w2b = wbig[0:64, 41984:43008]
            a2b = wbig[0:64, 43008:44032]
            hb1 = al("hb1", [128, D], F32)
            stg0 = al("F0", [128, 8, 128], F32)
            stg1 = al("F1", [128, 8, 128], F32)
            stg_tiles[:] = [(stg0[:].rearrange("p c t -> p (c t)"), "F0"), (stg1[:].rearrange("p c t -> p (c t)"), "F1")]
            load_wk(rw_in, rwkv_w_in, 4096, 8, gl=1)
            load_wk(rw_out, rwkv_w_out, 1024, 8)
            load_wk(w1b, rwkv_w1, 64, 8, gl=1)
            load_wk(a1b, rwkv_a1, 64, 8, gl=1)
            load_w(w2b, rwkv_w2, 1024, rows=64)
            load_w(a2b, rwkv_a2, 1024, rows=64)

            NCST = 12
            cst = al("cst", [128, NCST, 8], F32)
            srcs = [rwkv_mu[i] for i in range(6)] + [rwkv_a0, rwkv_k_k, rwkv_k_a, rwkv_r_k, rwkv_gn_g, rwkv_gn_b]
            for i, s in enumerate(srcs):
                P.dma(cst[:, i, :], s.rearrange("(c p) -> p c", p=128), (), ["cst"], slow=True)
            C_A0, C_KK, C_KA, C_RK, C_GG, C_GB = 6, 7, 8, 9, 10, 11

            def cbc(i):
                return cst[:, i, :].unsqueeze(2).broadcast_to([128, 8, 128])
            w0bc = al("w0bc", [128, D], F32)
            P.dma(w0bc[:], rwkv_w0.partition_broadcast(128), (), ["w0bc"])
            tri2 = al("tri2", [128, 2, 128], BF16)
            P.cp("dve", tri2[:, 0, :], maskb[:, 1, :], ["maskb"], ["tri2"])
            P.cp("dve", tri2[:, 1, :], maskb[:, 0, :], ["maskb"], ["tri2"])
            bones = al("bones", [128, 128], BF16)
            P.cp("dve", bones[:], maskb[:, 3, :], ["maskb"], ["bones"])

            FM = [128, 8, 128]
            ssq = al("ssq1", [128, 4], F32)
            FB = [stg0, stg1] + [al("F%d" % i, FM, F32) for i in range(2, 11)]
            BB = [al("B%d" % i, FM, BF16) for i in range(3)]
            P.alias = {"dd": "F0", "kkn": "F0", "ltmp": "F1", "Ep": "F1", "Elag": "F1", "rT": "F2", "G": "B1",
                       "kTf": "F3", "yo": "F3", "vT": "F4", "aT": "F5", "sig": "F6", "Ec": "F6", "Em": "F7",
                       "f1": "F8", "f2": "F9", "bonT": "F10", "ybf": "B0", "gTb": "B0", "xs0": "B1", "fmb": "B1",
                       "xs1": "B2", "QT": "B2"}
            dd = kkn = FB[0]
            ltmp = Ep = Elag = FB[1]
            rT = FB[2]
            G = BB[1]
            sqb = BB[0]
            sgh = BB[1][:].rearrange("p c t -> p (c t)")
            sgl = BB[2][:].rearrange("p c t -> p (c t)")
            kTf = yo = FB[3]
            vT, aT, Em, f1, f2, bonT = FB[4], FB[5], FB[7], FB[8], FB[9], FB[10]
            Ec = FB[6]
            sig = FB[6][:].rearrange("p c t -> p (c t)")
            ybf = BB[0][:].rearrange("p c t -> p (c t)")
            gTb = BB[0]
            xs = [BB[1], BB[2]]
            fmb, QT = BB[1], BB[2]
            yT = [al("yT%d" % i, [128, 8, 129], BF16) for i in range(2)]
            szT = al("szT", FM, BF16)
            tanhT = al("tanhT", [64, 128], BF16)
            amid = al("amid", [64, 128], BF16)
            gam = al("gam", [128, 8], F32)
            ART = al("ART", [128, 8, 2, 128], BF16)
            BtT = al("BtT", FM, BF16)
            KtT = al("KtT", FM, BF16)
            BhP4 = al("BhP4", [128, 8, 256], BF16)
            KhP4 = al("KhP4", [128, 8, 256], BF16)
            BVA = al("BVA", [128, NH, 2, 64], BF16)
            UP4 = al("UP4", [128, 8, 256], BF16)
            WP4 = al("WP4", [128, 8, 256], BF16)
            VP4 = al("VP4", [128, 8, 256], BF16)
            for nm, tns in (("UP4", UP4), ("WP4", WP4), ("VP4", VP4), ("BhP4", BhP4), ("KhP4", KhP4)):
                P.memset("pool", tns[:], 0.0, [nm])

            def p4v(tns):
                return tns[:].rearrange("p c (e s k) -> p c e s k", e=2, s=2)
            Xb = [al("Xb%d" % i, [128, 4, 128], BF16) for i in range(2)]
            Zb = [al("Zb%d" % i, [128, 4, 128], BF16) for i in range(2)]
            Pb = [al("Pb%d" % i, [128, 4, 128], BF16) for i in range(2)]
            ArT = al("ArT", [128, 4, 128], BF16)
            BT = al("BT", [128, 4, 128], BF16)
            BrT = al("BrT", [128, 4, 128], BF16)
            N2 = al("N2", [128, 8, 64], F32)
            S = al("S", [128, 8, 64], F32)
            Sh = al("Sh", [128, 8, 64], BF16)
            Sl = al("Sl", [128, 8, 64], BF16)
            Sbd = al("Sbd", [128, 8, 128], BF16)
            P.memset("pool", S[:], 0.0, ["S"])
            P.memset("pool", Sbd[:], 0.0, ["Sbd"])
            for i in range(2):
                P.memset("pool", yT[i][:], 0.0, ["yT%d" % i])

            for t in range(NT):
                par = t % 2
                h = hb1
                Rh = ["hb1"]
                P.dma(h[:], H1[t * 128:(t + 1) * 128, :], ["H1_%d" % t], Rh)
                for _once in (0,):
                    rms_stats("act", h[:], Rh, ssq, ltmp[:].rearrange("p c t -> p (c t)"), "ltmp")
                    P.ts("dve", ybf[:], h[:], ssq[:, 2:3], ALU.mult, Rh + ["ssq"], ["ybf"])
                    pb = bank(7)[:, 0, :].bitcast(BF16).rearrange("p (c t) -> p c t", c=8)
                    for c in range(8):
                        P.tr(pb[:, c, :], ybf[:, c * 128:(c + 1) * 128], identb[:], ["ybf", "identb"], bres(7))
                    yTc = yT[par]
                    Ry = ["yT%d" % par]
                    P.cp("dve", yTc[:, :, 1:129], pb, bres(7), Ry)
                    P.cp("dve", yT[1 - par][:, :, 0:1], yTc[:, :, 128:129], Ry, ["yT%d" % (1 - par)])
                    P.tt("dve", dd[:], yTc[:, :, 0:128], yTc[:, :, 1:129], ALU.subtract, Ry, ["dd"])

                    def lerp(ci, dst, dres):
                        P.tt("dve", ltmp[:], dd[:], cbc(ci), ALU.mult, ["dd", "cst"], ["ltmp"])
                        P.tt("dve", dst[:], ltmp[:], yTc[:, :, 1:129], ALU.add, ["ltmp"] + Ry, [dres])

                    def proj_fm(ci, xsrc, xres, b):
                        pp = bank(b, 2).rearrange("p a (c t) -> p (a c) t", c=4)
                        for oc in range(8):
                            for kc in range(8):
                                P.mm(pp[:, oc, :], rw_in[:, kc, ci * 1024 + oc * 128: ci * 1024 + (oc + 1) * 128],
                                     xsrc[:, kc, :], kc == 0, kc == 7, [xres, "W"], bres(b + oc // 4))
                        return pp

                    lerp(0, xs[0], "xs0")
                    pp = proj_fm(0, xs[0], "xs0", 0)
                    P.cp("act", rT[:], pp, bres(0, 2), ["rT"])
                    lerp(1, xs[1], "xs1")
                    pp = proj_fm(1, xs[1], "xs1", 2)
                    P.cp("dve", kTf[:], pp, bres(2, 2), ["kTf"])
                    lerp(2, xs[0], "xs0")
                    pp = proj_fm(2, xs[0], "xs0", 4)
                    P.cp("act", vT[:], pp, bres(4, 2), ["vT"])
                    lerp(3, xs[1], "xs1")
                    pp = proj_fm(3, xs[1], "xs1", 0)
                    P.act(szT[:], pp, AF.Silu, bres(0, 2), ["szT"])
                    if stop == 1:
                        break
                    lerp(4, xs[0], "xs0")
                    lw = bank(6)[0:64, 0, 0:128]
                    for kc in range(8):
                        P.mm(lw, w1b[:, kc, :], xs[0][:, kc, :], kc == 0, kc == 7, ["xs0", "W"], bres(6))
                    P.act(tanhT[:], lw, AF.Tanh, bres(6), ["tanhT"])
                    u2 = bank(2, 2)
                    for half in range(2):
                        P.mm(u2[:, half, :], tanhT[:], w2b[:, half * 512:(half + 1) * 512], True, True, ["tanhT", "W"],
                             bres(2 + half))
                    P.tt("dve", sig[:].rearrange("p (a n) -> p a n", a=2), u2, w0bc[:].rearrange("p (a n) -> p a n", a=2),
                         ALU.add, bres(2, 2) + ["w0bc"], ["sig"])
                    P.act(sig[:], sig[:], AF.Sigmoid, ["sig"], ["sig"])
                    lerp(5, xs[1], "xs1")
                    la = bank(6)[0:64, 0, 128:256]
                    for kc in range(8):
                        P.mm(la, a1b[:, kc, :], xs[1][:, kc, :], kc == 0, kc == 7, ["xs1", "W"], bres(6))
                    P.cp("act", amid[:], la, bres(6), ["amid"])
                    ap2 = bank(4, 2).rearrange("p a (c t) -> p (a c) t", c=4)
                    for oc in range(8):
                        P.mm(ap2[:, oc, :], a2b[:, oc * 128:(oc + 1) * 128], amid[:], True, True, ["amid", "W"], bres(4 + oc // 4))
                    P.tt("dve", aT[:], ap2, cbc(C_A0), ALU.add, bres(4, 2) + ["cst"], ["aT"])
                    P.act(aT[:], aT[:], AF.Sigmoid, ["aT"], ["aT"])
                    if stop == 2:
                        break
                    P.tt("dve", kkn[:], kTf[:], cbc(C_KK), ALU.mult, ["kTf", "cst"], ["kkn"])
                    P.tt("pool", sqb[:], kkn[:], kkn[:], ALU.mult, ["kkn"], ["B0"])
                    s2 = bank(4, 2)
                    for half in range(2):
                        P.mm(s2[:, half, :], bones[:], sqb[:, half * 4:(half + 1) * 4, :].rearrange("p c t -> p (c t)"),
                             True, True, ["B0", "bones"], bres(4 + half))
                    s2v = s2.rearrange("p a (c t) -> p (a c) t", c=4)
                    P.act(f2[:], s2v, AF.Sqrt, bres(4, 2), ["f2"])
                    P.ts("dve", f2[:], f2[:], 1e-12, ALU.max, ["f2"], ["f2"])
                    P.op("dve", lambda e: e.reciprocal(out=f2[:], in_=f2[:]), ["f2"], ["f2"])
                    P.tt("pool", kkn[:], kkn[:], f2[:], ALU.mult, ["kkn", "f2"], ["kkn"])
                    P.stt("dve", f1[:], aT[:], -1.0, cbc(C_KA), ALU.add, ALU.mult, ["aT", "cst"], ["f1"])
                    P.stt("dve", f1[:], f1[:], 1.0, kTf[:], ALU.add, ALU.mult, ["f1", "kTf"], ["f1"])
                    P.tt("pool", f2[:], kkn[:], aT[:], ALU.mult, ["kkn", "aT"], ["f2"])
                    lpvs = []
                    P.cp("pool", sgh, sig, ["sig"], ["B1"])
                    P.tt("dve", sgl, sig, sgh, ALU.subtract, ["sig", "B1"], ["B2"])
                    for half in range(2):
                        lb = 0 + 2 * half
                        lp = bank(lb, 2).rearrange("p a (c n) -> p (a c) n", c=2)
                        for c4 in range(4):
                            ch = half * 4 + c4
                            P.mm(lp[:, c4, :], sgh[:, ch * 128:(ch + 1) * 128], tri2[:].rearrange("p a n -> p (a n)"),
                                 True, False, ["B1", "tri2"], bres(lb + c4 // 2))
                            P.mm(lp[:, c4, :], sgl[:, ch * 128:(ch + 1) * 128], tri2[:].rearrange("p a n -> p (a n)"),
                                 False, True, ["B2", "tri2"], bres(lb + c4 // 2))
                        lpvs.append(lp.rearrange("p c (a n) -> p c a n", a=2))
                    for half in range(2):
                        sl = slice(half * 4, half * 4 + 4)
                        P.act(Ep[:, sl, :], lpvs[half][:, :, 0, :], AF.Exp, bres(2 * half, 2), ["Ep"], scale=DECAY_K)
                        P.act(Em[:, sl, :], lpvs[half][:, :, 0, :], AF.Exp, bres(2 * half, 2), ["Em"], scale=-DECAY_K)
                    P.cp("dve", gam[:].unsqueeze(2), Ep[:, :, 127:128], ["Ep"], ["gam"])
                    P.tt("pool", ART[:, :, 1, :], rT[:], Ep[:], ALU.mult, ["rT", "Ep"], ["ART"])
                    for half in range(2):
                        sl = slice(half * 4, half * 4 + 4)
                        P.act(Elag[:, sl, :], lpvs[half][:, :, 1, :], AF.Exp, bres(2 * half, 2), ["Elag"], scale=DECAY_K)
                    P.stt("dve", ART[:, :, 0, :], kkn[:], -1.0, Elag[:], ALU.mult, ALU.mult, ["kkn", "Elag"], ["ART"])
                    P.tt("dve", Ec[:], Em[:], gam[:].unsqueeze(2).broadcast_to(FM), ALU.mult, ["Em", "gam"], ["Ec"])
                    P.tt("pool", BtT[:], f2[:], Em[:], ALU.mult, ["f2", "Em"], ["BtT"])
                    P.tt("pool", KtT[:], f1[:], Em[:], ALU.mult, ["f1", "Em"], ["KtT"])
                    P.tt("dve", ltmp[:], rT[:], cbc(C_RK), ALU.mult, ["rT", "cst"], ["ltmp"])
                    P.tt("pool", sqb[:], ltmp[:], f1[:], ALU.mult, ["ltmp", "f1"], ["B0"])
                    b2 = bank(6, 2)
                    for half in range(2):
                        P.mm(b2[:, half, :], bones[:], sqb[:, half * 4:(half + 1) * 4, :].rearrange("p c t -> p (c t)"),
                             True, True, ["B0", "bones"], bres(6 + half))
                    P.tt("dve", bonT[:], b2.rearrange("p a (c t) -> p (a c) t", c=4), vT[:], ALU.mult, bres(6, 2) + ["vT"], ["bonT"])

                    if stop == 3:
                        break
                    def to_tok(src_fm, Rs, dst_view, Wd, b, eng, split=False):
                        pbk = bank(b)[:, 0, :].bitcast(BF16).rearrange("p (c t) -> p c t", c=8)
                        for c in range(8):
                            P.tr(pbk[:, c, :], src_fm[:, c, :], identb[:], Rs + ["identb"], bres(b))
                        srcv = pbk.rearrange("p c (e k) -> p c e k", e=2) if split else pbk
                        P.cp(eng, dst_view, srcv, bres(b), Wd)
                    to_tok(ART[:, :, 0, :], ["ART"], BVA[:].rearrange("p (c e) s k -> p c e s k", e=2)[:, :, :, 1, :],
                           ["BVA"], 4, "act", split=True)
                    P.tt("pool", fmb[:], f2[:], Ec[:], ALU.mult, ["f2", "Ec"], ["fmb"])
                    to_tok(fmb, ["fmb"], p4v(BhP4)[:, :, :, 0, :], ["BhP4"], 5, "act", split=True)
                    P.tt("pool", fmb[:], f1[:], Ec[:], ALU.mult, ["f1", "Ec"], ["fmb"])
                    to_tok(fmb, ["fmb"], p4v(KhP4)[:, :, :, 0, :], ["KhP4"], 4, "act", split=True)
                    P.cp("pool", fmb[:], vT[:], ["vT"], ["fmb"])
                    to_tok(fmb, ["fmb"], p4v(VP4)[:, :, :, 0, :], ["VP4"], 5, "act", split=True)

                    if stop == 4:
                        break
                    for hg in range(4):
                        heads = [4 * hg + i for i in range(4)]
                        aar = bank(0, 2).rearrange("p a (h n) -> p (a h) n", h=2)
                        bbr = bank(2, 2).rearrange("p a (h n) -> p (a h) n", h=2)
                        afp = bank(4)[:, 0, :].rearrange("p (h n) -> p h n", h=4)
                        for i, hh in enumerate(heads):
                            c, e = hh // 2, hh % 2
                            ps = slice(64 * e, 64 * e + 64)
                            arr = ART[ps, c, :, :].rearrange("p a n -> p (a n)")
                            P.mm(aar[:, i, :], BtT[ps, c, :], arr, True, True, ["BtT", "ART"], bres(0 + i // 2), serial=True)
                            P.mm(bbr[:, i, :], KtT[ps, c, :], arr, True, True, ["KtT", "ART"], bres(2 + i // 2), serial=True)
                            P.mm(afp[:, i, :], ART[ps, c, 0, :], BtT[ps, c, :], True, True, ["BtT", "ART"], bres(4), serial=True)
                        aarv = aar.rearrange("p h (a n) -> p h a n", a=2)
                        bbrv = bbr.rearrange("p h (a n) -> p h a n", a=2)

                        def mb(m):
                            return m.unsqueeze(1).broadcast_to([128, 4, 128])
                        X, Z, Pm = Xb[0], Zb[0], Pb[0]
                        P.tt("dve", X[:], aarv[:, :, 0, :], mb(MUs), ALU.mult, bres(0, 2) + ["maskb"], ["Xb0"])
                        P.tt("dve", ArT[:], aarv[:, :, 1, :], mb(MUi), ALU.mult, bres(0, 2) + ["maskb"], ["ArT"])
                        P.tt("dve", BT[:], bbrv[:, :, 0, :], mb(MUs), ALU.mult, bres(2, 2) + ["maskb"], ["BT"])
                        P.tt("dve", BrT[:], bbrv[:, :, 1, :], mb(MUi), ALU.mult, bres(2, 2) + ["maskb"], ["BrT"])
                        P.tt("dve", Z[:], afp, mb(MLs), ALU.mult, bres(4) + ["maskb"], ["Zb0"])
                        P.tt("pool", Pm[:], X[:], mb(identb[:]), ALU.add, ["Xb0", "identb"], ["Pb0"])
                        bvp = bank(5)[:, 0, 0:256].rearrange("p (h n) -> p h n", h=4)
                        for i, hh in enumerate(heads):
                            c, e = hh // 2, hh % 2
                            P.mm(bvp[:, i, :], BT[:, i, :], p4v(VP4)[:, c, e, 0, :], True, True, ["BT", "VP4"], bres(5))
                        P.cp("act", BVA[:, 4 * hg:4 * hg + 4, 0, :], bvp, bres(5), ["BVA"])
                        cur = 0
                        for lvl in range(1, 7):
                            nxt = 1 - cur
                            Xc, Zc, Pc = Xb[cur], Zb[cur], Pb[cur]
                            Xn, Zn, Pn = Xb[nxt], Zb[nxt], Pb[nxt]
                            rc = ["Xb%d" % cur, "Zb%d" % cur]
                            zp = bank(6)[:, 0, :].rearrange("p (h n) -> p h n", h=4)
                            for i in range(4):
                                P.mm(zp[:, i, :], Xc[:, i, :], Zc[:, i, :], True, True, rc, bres(6))
                            P.cp("act", Zn[:], zp, bres(6), ["Zb%d" % nxt])
                            if lvl < 6:
                                xp = bank(7)[:, 0, :].rearrange("p (h n) -> p h n", h=4)
                                for i in range(4):
                                    P.mm(xp[:, i, :], Zc[:, i, :], Xc[:, i, :], True, True, rc, bres(7))
                                P.cp("dve", Xn[:], xp, bres(7), ["Xb%d" % nxt])
                            pp_ = bank(4)[:, 0, :].rearrange("p (h n) -> p h n", h=4)
                            for i in range(4):
                                P.mm(pp_[:, i, :], Zn[:, i, :], Pc[:, i, :], True, True, ["Zb%d" % nxt, "Pb%d" % cur], bres(4))
                            P.tt("dve", Pn[:], pp_, Pc[:], ALU.add, bres(4) + ["Pb%d" % cur], ["Pb%d" % nxt])
                            cur = nxt
                        Pf = Pb[cur]
                        uwp = bank(5)[:, 0, :].rearrange("p (h n) -> p h n", h=4)
                        for i, hh in enumerate(heads):
                            P.mm(uwp[:, i, :], Pf[:, i, :], BVA[:, hh, :, :].rearrange("p s k -> p (s k)"), True, True,
                                 ["Pb%d" % cur, "BVA"], bres(5))
                        uwv = uwp.rearrange("p (c e) (s k) -> p c e s k", e=2, s=2)
                        P.cp("act", p4v(UP4)[:, 2 * hg:2 * hg + 2, :, 0, :], uwv[:, :, :, 0, :], bres(5), ["UP4"])
                        P.cp("dve", p4v(WP4)[:, 2 * hg:2 * hg + 2, :, 0, :], uwv[:, :, :, 1, :], bres(5), ["WP4"])
                        qp = bank(6)[:, 0, 0:256].rearrange("p (c n) -> p c n", c=2)
                        for j in range(2):
                            c = 2 * hg + j
                            for e in range(2):
                                P.mm(qp[:, j, :], WP4[:, c, 64 * e:64 * e + 128], ArT[:, 2 * j + e, :], e == 0, e == 1,
                                     ["WP4", "ArT"], bres(6))
                        P.tt("dve", QT[:, 2 * hg:2 * hg + 2, :], qp, ART[:, 2 * hg:2 * hg + 2, 1, :], ALU.add,
                             bres(6) + ["ART"], ["QT"])
                        n2p = bank(7)[:, 0, 0:128].rearrange("p (c n) -> p c n", c=2)
                        gp = bank(7)[:, 0, 256:512].rearrange("p (c n) -> p c n", c=2)
                        for j in range(2):
                            c = 2 * hg + j
                            for e in range(2):
                                P.mm(n2p[:, j, :], BhP4[:, c, 64 * e:64 * e + 128], p4v(UP4)[:, c, e, 0, :], e == 0, False,
                                     ["BhP4", "UP4"], bres(7))
                                P.mm(n2p[:, j, :], KhP4[:, c, 64 * e:64 * e + 128], p4v(VP4)[:, c, e, 0, :], False, e == 1,
                                     ["KhP4", "VP4"], bres(7))
                            for e in range(2):
                                P.mm(gp[:, j, :], WP4[:, c, 64 * e:64 * e + 128], BhP4[:, c, 64 * e:64 * e + 128], e == 0, e == 1,
                                     ["WP4", "BhP4"], bres(7))
                        P.cp("act", N2[:, 2 * hg:2 * hg + 2, :], n2p, bres(7), ["N2"])
                        P.cp("dve", G[:, 2 * hg:2 * hg + 2, :], gp, bres(7), ["G"])
                        yp = bank(6)[:, 0, 256:512].rearrange("p (c n) -> p c n", c=2)
                        for j in range(2):
                            c = 2 * hg + j
                            for e in range(2):
                                P.mm(yp[:, j, :], UP4[:, c, 64 * e:64 * e + 128], ArT[:, 2 * j + e, :], e == 0, False,
                                     ["UP4", "ArT"], bres(6))
                                P.mm(yp[:, j, :], VP4[:, c, 64 * e:64 * e + 128], BrT[:, 2 * j + e, :], False, False,
                                     ["VP4", "BrT"], bres(6))
                            P.mm(yp[:, j, :], Sbd[:, c, :], QT[:, c, :], False, True, ["Sbd", "QT"], bres(6))
                        P.cp("act", yo[:, 2 * hg:2 * hg + 2, :], yp, bres(6), ["yo"])
                    if stop == 5:
                        break
                    sp_ = bank(0)[:, 0, :].rearrange("p (c v) -> p c v", c=8)
                    P.cp("act", Sh[:], S[:], ["S"], ["Sh"])
                    P.tt("dve", Sl[:], S[:], Sh[:], ALU.subtract, ["S", "Sh"], ["Sl"])
                    for c in range(8):
                        P.mm(sp_[:, c, :], G[:, c, :], Sh[:, c, :], True, False, ["G", "Sh"], bres(0))
                        P.mm(sp_[:, c, :], G[:, c, :], Sl[:, c, :], False, True, ["G", "Sl"], bres(0))
                    P.tt("dve", S[:], S[:], gam[:].unsqueeze(2).broadcast_to([128, 8, 64]), ALU.mult, ["S", "gam"], ["S"])
                    P.tt("pool", S[:], S[:], N2[:], ALU.add, ["S", "N2"], ["S"])
                    P.tt("dve", S[:], S[:], sp_, ALU.add, ["S"] + bres(0), ["S"])
                    P.cp("act", Sbd[0:64, :, 0:64], S[0:64, :, :], ["S"], ["Sbd"])
                    P.cp("act", Sbd[64:128, :, 64:128], S[64:128, :, :], ["S"], ["Sbd"])
                    if stop == 6:
                        break
                    m2 = bank(1, 2)
                    P.cp("pool", sqb[:], yo[:], ["yo"], ["B0"])
                    for half in range(2):
                        P.mm(m2[:, half, :], bones[:], sqb[:, half * 4:(half + 1) * 4, :].rearrange("p c t -> p (c t)"),
                             True, True, ["B0", "bones"], bres(1 + half))
                    P.stt("dve", f1[:], m2.rearrange("p a (c t) -> p (a c) t", c=4), -1.0 / 64, yo[:], ALU.mult, ALU.add,
                          bres(1, 2) + ["yo"], ["f1"])
                    P.tt("pool", sqb[:], f1[:], f1[:], ALU.mult, ["f1"], ["B0"])
                    v2 = bank(3, 2)
                    for half in range(2):
                        P.mm(v2[:, half, :], bones[:], sqb[:, half * 4:(half + 1) * 4, :].rearrange("p c t -> p (c t)"),
                             True, True, ["B0", "bones"], bres(3 + half))
                    P.act(f2[:], v2.rearrange("p a (c t) -> p (a c) t", c=4), AF.Sqrt, bres(3, 2), ["f2"], scale=1.0 / 64,
                          bias=64e-5)
                    P.op("dve", lambda e: e.reciprocal(out=f2[:], in_=f2[:]), ["f2"], ["f2"])
                    P.tt("pool", f1[:], f1[:], f2[:], ALU.mult, ["f1", "f2"], ["f1"])
                    P.tt("dve", f1[:], f1[:], cbc(C_GG), ALU.mult, ["f1", "cst"], ["f1"])
                    P.tt("dve", f1[:], f1[:], cbc(C_GB), ALU.add, ["f1", "cst"], ["f1"])
                    P.tt("dve", f1[:], f1[:], bonT[:], ALU.add, ["f1", "bonT"], ["f1"])
                    P.tt("pool", gTb[:], f1[:], szT[:], ALU.mult, ["f1", "szT"], ["gTb"])
                    o2 = bank(1, 2)
                    for half in range(2):
                        for kc in range(8):
                            P.mm(o2[:, half, :], gTb[:, kc, :], rw_out[:, kc, half * 512:(half + 1) * 512], kc == 0, kc == 7,
                                 ["gTb", "W"], bres(1 + half))
                    P.tt("dve", h[:].rearrange("p (a n) -> p a n", a=2), h[:].rearrange("p (a n) -> p a n", a=2), o2, ALU.add,
                         Rh + bres(1, 2), Rh)
                P.dma(H2[t * 128:(t + 1) * 128, :], h[:], Rh, ["H2_%d" % t])
            P.barrier()

    if "l1b" in phases:
        with contextlib.ExitStack() as st:
            aoff[0] = 0
            w_g = wbig[:, 0:8192].rearrange("p (c n) -> p c n", c=8)
            w_p = wbig[:, 8192:10240].rearrange("p (c n) -> p c n", c=2)
            hbuf = [al("hb2_%d" % i, [128, D], F32) for i in range(2)]
            stg_tiles[:] = [(hbuf[i], "hb2_%d" % i) for i in range(2)]
            load_wk(w_g, ple_w_gate[1], 1024, 8)
            load_wk(w_p, ple_w_proj[1], 1024, 2)
            fg = al("fg", [128, D], F32)
            P.dma(fg[:], final_norm_g.partition_broadcast(128), (), ["fg"])
            ptl = [al("pt2_%d" % i, [128, PLE], F32) for i in range(2)]
            ssq = al("ssq2", [128, 4], F32)
            junk = al("junk2", [128, D], F32)
            wk = dict(hb=al("hb_b", [128, D], BF16), hT=al("hT_b", [128, 8, 128], BF16), pb=al("pb_b", [128, PLE], BF16),
                      pT=al("pT_b", [128, 2, 128], BF16), sg=al("sg_b", [128, D], F32))
            src = H2 if "l1a" in phases else H1

            def loads(t):
                par = t % 2
                P.dma(hbuf[par][:], src[t * 128:(t + 1) * 128, :], ["H2_%d" % t if "l1a" in phases else "H1_%d" % t],
                      ["hb2_%d" % par])
                P.dma(ptl[par][:], p1[t * 128:(t + 1) * 128, :], (), ["pt2_%d" % par])
            loads(0)
            for t in range(NT):
                par = t % 2
                if t + 1 < NT:
                    loads(t + 1)
                h = hbuf[par]
                Rh = ["hb2_%d" % par]
                ple_part(h[:], Rh, ptl[par][:], ["pt2_%d" % par], w_g, w_p, wk, 0, 2, 4)
                rms_stats("act", h[:], Rh, ssq, junk[:], "junk2")
                P.stt("dve", h[:], h[:], ssq[:, 2:3], fg[:], ALU.mult, ALU.mult, Rh + ["ssq", "fg"], Rh)
                P.dma(out[t * 128:(t + 1) * 128, :], h[:], Rh, ["out_%d" % t])
            P.wait_all("sp", ["out_%d" % t for t in range(NT)])
    P.emit()
    glob.close()
    return nc


def host_consts(T):
    idx = np.arange(128)
    pp, ff = idx[:, None], idx[None, :]
    masks = np.stack([(pp < ff), (pp <= ff), (ff < pp), (pp // 64 == ff // 64)], axis=1).astype(np.float32)
    half = 32
    inv = 10000.0 ** (-np.arange(half, dtype=np.float32) / half)
    ang = np.arange(T, dtype=np.float32)[:, None] * inv[None, :]
    rope = np.stack([np.cos(ang), np.sin(ang)], axis=1).astype(np.float32)
    return {"c_ident": np.eye(128, dtype=np.float32), "c_masks": np.ascontiguousarray(masks),
            "c_rope": np.ascontiguousarray(rope)}


def make_in_maps(inputs, T, B):
    f = lambda a: np.ascontiguousarray(np.asarray(a, dtype=np.float32))
    shared = {
        "norm_g": f(inputs["norm_g"]), "attn_w_in": f(inputs["attn_w_in"][0]), "attn_b_in": f(inputs["attn_b_in"][0]),
        "attn_sinks": f(inputs["attn_sinks"][0]), "attn_w_out": f(inputs["attn_w_out"][0]),
        "rwkv_mu": f(inputs["rwkv_mu"][0]), "rwkv_w_in": f(inputs["rwkv_w_in"][0]), "rwkv_w0": f(inputs["rwkv_w0"][0]),
        "rwkv_w1": f(inputs["rwkv_w1"][0]), "rwkv_w2": f(inputs["rwkv_w2"][0]), "rwkv_a0": f(inputs["rwkv_a0"][0]),
        "rwkv_a1": f(inputs["rwkv_a1"][0]), "rwkv_a2": f(inputs["rwkv_a2"][0]), "rwkv_k_k": f(inputs["rwkv_k_k"][0]),
        "rwkv_k_a": f(inputs["rwkv_k_a"][0]), "rwkv_r_k": f(inputs["rwkv_r_k"][0]).reshape(-1),
        "rwkv_gn_g": f(inputs["rwkv_gn_g"][0]), "rwkv_gn_b": f(inputs["rwkv_gn_b"][0]),
        "rwkv_w_out": f(inputs["rwkv_w_out"][0]), "ple_w_proj": f(inputs["ple_w_proj"]),
        "ple_w_gate": f(inputs["ple_w_gate"]), "final_norm_g": f(inputs["final_norm_g"]),
    }
    shared.update(host_consts(T))
    xs = f(inputs["x"])
    ps = f(inputs["p"])
    maps = []
    for b in range(B):
        m = dict(shared)
        m["x"] = xs[b]
        m["p0"] = ps[0, b]
        m["p1"] = ps[1, b]
        maps.append(m)
    return maps


_NC_CACHE = {}


def kernel(**inputs):
    x = np.asarray(inputs["x"])
    B, T, _ = x.shape
    if T not in _NC_CACHE:
        _NC_CACHE[T] = build_program(T)
    nc = _NC_CACHE[T]
    maps = make_in_maps(inputs, T, B)
    res = run_bass_kernel_spmd(nc, maps, core_ids=list(range(B)))
    return np.stack([np.asarray(r["out"], dtype=np.float32) for r in res.results], axis=0)
```
